# Optimizing a Trainium2 kernel written in Bass

```python
import math
import jax, jax.numpy as jnp
from jax import lax
import numpy as np

D_MODEL = 1024
BATCH = 2
SEQ = 16384
DEPTH = 4

N_EVEN = (DEPTH + 1) // 2
N_ODD = DEPTH // 2
RMS_EPS = 1e-6
D_FF = -(-8 * D_MODEL // (3 * 256)) * 256
MIX_WIDTH = D_MODEL

POOL_WIDTH = MIX_WIDTH // 4
POOL_WINDOWS = (2, 4, 8, 16)
POOL_GROUPS = len(POOL_WINDOWS)
POOL_GROUP_DIM = POOL_WIDTH // POOL_GROUPS
SSD_WIDTH = MIX_WIDTH - POOL_WIDTH
SSD_HEAD_DIM = 64
SSD_HEADS = SSD_WIDTH // SSD_HEAD_DIM
SSD_GROUPS = 2
SSD_HEADS_PER_GROUP = SSD_HEADS // SSD_GROUPS
SSD_STATE = 128
SSD_CONV = 4
SSD_CHUNK = 128
SSD_BC_DIM = SSD_GROUPS * SSD_STATE
SSD_CONV_DIM = SSD_WIDTH + 2 * SSD_BC_DIM
EVEN_IN = POOL_WIDTH + SSD_WIDTH + SSD_CONV_DIM + SSD_HEADS

RWKV_WIDTH = MIX_WIDTH // 2
RWKV_HEAD_DIM = 64
RWKV_HEADS = RWKV_WIDTH // RWKV_HEAD_DIM
RWKV_DECAY_RANK = 64
RWKV_ICLR_RANK = 64
RWKV_GATE_RANK = 128
RWKV_IN = 3 * RWKV_WIDTH + RWKV_DECAY_RANK + RWKV_ICLR_RANK + RWKV_GATE_RANK
RWKV_GN_EPS = 64e-5
RWKV_DECAY_OFFSET = 0.5
LRU_WIDTH = MIX_WIDTH - RWKV_WIDTH
LRU_BLOCKS = 8
LRU_BLOCK_DIM = LRU_WIDTH // LRU_BLOCKS
LRU_CONV = 4
LRU_C = 8.0
ODD_IN = RWKV_IN + 2 * LRU_WIDTH

kernel_name = "hybrid_pool_ssd_rwkv7_rglru_trunk"


def rmsnorm(x, g):
    xf = x.astype(jnp.float32)
    y = xf * lax.rsqrt(jnp.mean(xf * xf, axis=-1, keepdims=True) + RMS_EPS)
    return (y * g.astype(jnp.float32)).astype(x.dtype)


def causal_depthwise_conv(x, w, b):
    k = w.shape[0]
    y = lax.conv_general_dilated(x, w[:, None, :], window_strides=(1,), padding=((k - 1, 0),),
                                 dimension_numbers=("NWC", "WIO", "NWC"),
                                 feature_group_count=x.shape[-1])
    return y + b


def swiglu(h, w_gate, w_up, w_down):
    return (jax.nn.silu(h @ w_gate) * (h @ w_up)) @ w_down


def pool_mixer(u, pool_w, pool_scale):
    bsz, s, _ = u.shape
    uf = u.astype(jnp.float32).reshape(bsz, s, POOL_GROUPS, POOL_GROUP_DIM)
    cs = jnp.cumsum(uf, axis=1)
    cs = jnp.concatenate([jnp.zeros_like(cs[:, :1]), cs], axis=1)
    pos = jnp.arange(1, s + 1, dtype=jnp.float32)
    pooled = []
    for gi, w in enumerate(POOL_WINDOWS):
        c = cs[:, :, gi]
        lo = jnp.concatenate([jnp.zeros_like(c[:, :w - 1]), c[:, :s + 1 - w]], axis=1)
        cnt = jnp.minimum(pos, float(w))[None, :, None]
        pooled.append((c[:, 1:] - lo) / cnt)
    pooled = jnp.stack(pooled, axis=2)
    d = (pooled - uf).astype(u.dtype)
    y = jnp.einsum("bsgc,gcd->bsgd", d, pool_w).reshape(bsz, s, POOL_WIDTH)
    return y * pool_scale


def ssd_chunked_scan(x, dt, a, bm, cm):
    bsz, s = x.shape[:2]
    nc = s // SSD_CHUNK
    L = SSD_CHUNK
    xc = (x * dt[..., None]).reshape(bsz, nc, L, SSD_GROUPS, SSD_HEADS_PER_GROUP, SSD_HEAD_DIM)
    bc = bm.reshape(bsz, nc, L, SSD_GROUPS, SSD_STATE)
    cc = cm.reshape(bsz, nc, L, SSD_GROUPS, SSD_STATE)
    da = jnp.transpose((dt * a).reshape(bsz, nc, L, SSD_GROUPS, SSD_HEADS_PER_GROUP), (0, 1, 3, 4, 2))
    cum = jnp.cumsum(da, axis=-1)
    seg = cum[..., :, None] - cum[..., None, :]
    mask = jnp.tril(jnp.ones((L, L), dtype=bool))
    decay = jnp.where(mask, jnp.exp(jnp.minimum(seg, 0.0)), 0.0)
    cb = jnp.einsum("bclgn,bcsgn->bcgls", cc, bc)
    y_diag = jnp.einsum("bcgls,bcgels,bcsgep->bclgep", cb, decay, xc)
    decay_to_end = jnp.exp(cum[..., -1:] - cum)
    chunk_states = jnp.einsum("bclgn,bcgel,bclgep->bcgepn", bc, decay_to_end, xc)
    chunk_decay = jnp.exp(cum[..., -1])

    def step(h, inp):
        st, dec = inp
        return h * dec[..., None, None] + st, h

    h0 = jnp.zeros((bsz, SSD_GROUPS, SSD_HEADS_PER_GROUP, SSD_HEAD_DIM, SSD_STATE), jnp.float32)
    _, prev = lax.scan(step, h0, (jnp.moveaxis(chunk_states, 1, 0), jnp.moveaxis(chunk_decay, 1, 0)))
    prev = jnp.moveaxis(prev, 0, 1)
    y_off = jnp.einsum("bclgn,bcgepn,bcgel->bclgep", cc, prev, jnp.exp(cum))
    return (y_diag + y_off).reshape(bsz, s, SSD_GROUPS, SSD_HEADS_PER_GROUP, SSD_HEAD_DIM)


def ssd_mixer(z, xbc, dt_raw, conv_w, conv_b, dt_bias, a_log, d_skip, norm_g):
    bsz, s, _ = z.shape
    f32 = jnp.float32
    xbc = jax.nn.silu(causal_depthwise_conv(xbc, conv_w, conv_b)).astype(f32)
    xh = xbc[..., :SSD_WIDTH].reshape(bsz, s, SSD_GROUPS, SSD_HEADS_PER_GROUP, SSD_HEAD_DIM)
    bm = xbc[..., SSD_WIDTH:SSD_WIDTH + SSD_BC_DIM].reshape(bsz, s, SSD_GROUPS, SSD_STATE)
    cm = xbc[..., SSD_WIDTH + SSD_BC_DIM:].reshape(bsz, s, SSD_GROUPS, SSD_STATE)
    dt = jax.nn.softplus(dt_raw.astype(f32) + dt_bias.astype(f32)).reshape(bsz, s, SSD_GROUPS, SSD_HEADS_PER_GROUP)
    a = -jnp.exp(a_log.astype(f32)).reshape(SSD_GROUPS, SSD_HEADS_PER_GROUP)
    y = ssd_chunked_scan(xh, dt, a, bm, cm)
    y = y + d_skip.astype(f32).reshape(SSD_GROUPS, SSD_HEADS_PER_GROUP)[:, :, None] * xh
    gsz = SSD_WIDTH // SSD_GROUPS
    y = y.reshape(bsz, s, SSD_GROUPS, gsz) * jax.nn.silu(z.astype(f32)).reshape(bsz, s, SSD_GROUPS, gsz)
    y = y * lax.rsqrt(jnp.mean(y * y, axis=-1, keepdims=True) + RMS_EPS)
    return (y.reshape(bsz, s, SSD_WIDTH) * norm_g.astype(f32)).astype(z.dtype)


def token_shift(p, mu):
    prev = jnp.concatenate([jnp.zeros_like(p[:, :1]), p[:, :-1]], axis=1)
    return p + mu * (prev - p)


def rwkv7_scan(r, w, k, v, kk, a):
    def step(st, inp):
        r_t, w_t, k_t, v_t, kk_t, a_t = inp
        sa = jnp.einsum("bhvk,bhk->bhv", st, -kk_t)
        st = (st * w_t[:, :, None, :] + sa[..., None] * (kk_t * a_t)[:, :, None, :]
              + v_t[..., None] * k_t[:, :, None, :])
        return st, jnp.einsum("bhvk,bhk->bhv", st, r_t)

    bsz = r.shape[0]
    s0 = jnp.zeros((bsz, RWKV_HEADS, RWKV_HEAD_DIM, RWKV_HEAD_DIM), jnp.float32)
    xs = tuple(jnp.moveaxis(t, 1, 0) for t in (r, w, k, v, kk, a))
    _, y = lax.scan(step, s0, xs)
    return jnp.moveaxis(y, 0, 1)


def rwkv7_mixer(p, mu, w0, w_up, a0, a_up, g_up, k_k, k_a, r_k, ln_g, ln_b):
    bsz, s, _ = p.shape
    f32 = jnp.float32
    p = token_shift(p, mu)
    o1, o2, o3 = RWKV_WIDTH, 2 * RWKV_WIDTH, 3 * RWKV_WIDTH
    o4, o5 = o3 + RWKV_DECAY_RANK, o3 + RWKV_DECAY_RANK + RWKV_ICLR_RANK
    r, k, v = p[..., :o1], p[..., o1:o2], p[..., o2:o3]
    wd, ad, gd = p[..., o3:o4], p[..., o4:o5], p[..., o5:]
    w_log = -jax.nn.softplus(-(w0 + jnp.tanh(wd) @ w_up).astype(f32)) - RWKV_DECAY_OFFSET
    decay = jnp.exp(-jnp.exp(w_log))
    a = jax.nn.sigmoid((a0 + ad @ a_up).astype(f32))
    g = (jax.nn.sigmoid(gd) @ g_up).astype(f32)
    heads = lambda t: t.astype(f32).reshape(bsz, s, RWKV_HEADS, RWKV_HEAD_DIM)
    r, k, v, decay, a = heads(r), heads(k), heads(v), heads(decay), heads(a)
    kk = k * k_k.astype(f32).reshape(RWKV_HEADS, RWKV_HEAD_DIM)
    kk = kk * lax.rsqrt(jnp.sum(kk * kk, axis=-1, keepdims=True) + 1e-12)
    k = k * (1.0 + (a - 1.0) * k_a.astype(f32).reshape(RWKV_HEADS, RWKV_HEAD_DIM))
    y = rwkv7_scan(r, decay, k, v, kk, a)
    mean = jnp.mean(y, axis=-1, keepdims=True)
    var = jnp.mean(jnp.square(y - mean), axis=-1, keepdims=True)
    y = ((y - mean) * lax.rsqrt(var + RWKV_GN_EPS)).reshape(bsz, s, RWKV_WIDTH)
    y = y * ln_g.astype(f32) + ln_b.astype(f32)
    bonus = jnp.sum(r * k * r_k.astype(f32), axis=-1, keepdims=True) * v
    y = y + bonus.reshape(bsz, s, RWKV_WIDTH)
    return (y * g).astype(p.dtype)


def _linear_combine(left, right):
    a_l, b_l = left
    a_r, b_r = right
    return a_l * a_r, a_r * b_l + b_r


def rglru_mixer(gate, xb, conv_w, conv_b, wa, ba, wx, bx, lam):
    bsz, s, _ = xb.shape
    f32 = jnp.float32
    xb = causal_depthwise_conv(xb, conv_w, conv_b)
    xblk = xb.reshape(bsz, s, LRU_BLOCKS, LRU_BLOCK_DIM)
    rg = jax.nn.sigmoid((jnp.einsum("bshi,hij->bshj", xblk, wa).reshape(bsz, s, LRU_WIDTH) + ba).astype(f32))
    ig = jax.nn.sigmoid((jnp.einsum("bshi,hij->bshj", xblk, wx).reshape(bsz, s, LRU_WIDTH) + bx).astype(f32))
    log_a = -LRU_C * rg * jax.nn.softplus(-lam.astype(f32))
    a = jnp.exp(log_a)
    mult = jnp.sqrt(-jnp.expm1(2.0 * log_a))
    b = mult * ig * xb.astype(f32)
    _, h = lax.associative_scan(_linear_combine, (a, b), axis=1)
    return (h * jax.nn.gelu(gate.astype(f32))).astype(xb.dtype)


def even_mixer(h, w_in, w_out, pool_w, pool_scale, conv_w, conv_b, dt_bias, a_log, d_skip, norm_g):
    p = h @ w_in
    o1 = POOL_WIDTH
    o2 = o1 + SSD_WIDTH
    o3 = o2 + SSD_CONV_DIM
    y_pool = pool_mixer(p[..., :o1], pool_w, pool_scale)
    y_ssd = ssd_mixer(p[..., o1:o2], p[..., o2:o3], p[..., o3:], conv_w, conv_b, dt_bias, a_log, d_skip, norm_g)
    return jnp.concatenate([y_pool, y_ssd], axis=-1) @ w_out


def odd_mixer(h, w_in, w_out, mu, w0, w_up, a0, a_up, g_up, k_k, k_a, r_k, ln_g, ln_b,
              lconv_w, lconv_b, wa, ba, wx, bx, lam):
    p = h @ w_in
    y_rwkv = rwkv7_mixer(p[..., :RWKV_IN], mu, w0, w_up, a0, a_up, g_up, k_k, k_a, r_k, ln_g, ln_b)
    y_lru = rglru_mixer(p[..., RWKV_IN:RWKV_IN + LRU_WIDTH], p[..., RWKV_IN + LRU_WIDTH:],
                        lconv_w, lconv_b, wa, ba, wx, bx, lam)
    return jnp.concatenate([y_rwkv, y_lru], axis=-1) @ w_out


def setup_inputs(seed: int = 0) -> dict:
    key = jax.random.key(seed)
    ks = jax.random.split(key, 48)
    f32 = jnp.float32
    nrm = lambda i, shape, scale: scale * jax.random.normal(ks[i], shape, f32)
    unif = lambda i, shape, lo, hi: jax.random.uniform(ks[i], shape, f32, lo, hi)
    E, O = N_EVEN, N_ODD
    dt0 = jnp.exp(unif(16, (E, SSD_HEADS), math.log(1e-3), math.log(1e-1)))
    lru_a = unif(40, (O, LRU_WIDTH), 0.9, 0.999) ** (1.0 / LRU_C)
    return {
        "x": nrm(0, (BATCH, SEQ, D_MODEL), 1.0),
        "mix_norm_g": 1.0 + nrm(1, (DEPTH, D_MODEL), 0.05),
        "ffn_norm_g": 1.0 + nrm(2, (DEPTH, D_MODEL), 0.05),
        "ffn_w_gate": nrm(3, (DEPTH, D_MODEL, D_FF), D_MODEL ** -0.5),
        "ffn_w_up": nrm(4, (DEPTH, D_MODEL, D_FF), D_MODEL ** -0.5),
        "ffn_w_down": nrm(5, (DEPTH, D_FF, D_MODEL), 0.5 * D_FF ** -0.5),
        "final_norm_g": 1.0 + nrm(6, (D_MODEL,), 0.05),
        "ev_w_in": nrm(7, (E, D_MODEL, EVEN_IN), D_MODEL ** -0.5),
        "ev_w_out": nrm(8, (E, MIX_WIDTH, D_MODEL), 0.5 * MIX_WIDTH ** -0.5),
        "pool_w": nrm(9, (E, POOL_GROUPS, POOL_GROUP_DIM, POOL_GROUP_DIM), POOL_GROUP_DIM ** -0.5),
        "pool_scale": 1.0 + nrm(10, (E, POOL_WIDTH), 0.05),
        "ssd_conv_w": nrm(11, (E, SSD_CONV, SSD_CONV_DIM), 0.5),
        "ssd_conv_b": nrm(12, (E, SSD_CONV_DIM), 0.02),
        "ssd_dt_bias": dt0 + jnp.log(-jnp.expm1(-dt0)),
        "ssd_a_log": jnp.log(unif(13, (E, SSD_HEADS), 1.0, 16.0)),
        "ssd_d": 1.0 + nrm(14, (E, SSD_HEADS), 0.1),
        "ssd_norm_g": 1.0 + nrm(15, (E, SSD_WIDTH), 0.05),
        "od_w_in": nrm(20, (O, D_MODEL, ODD_IN), D_MODEL ** -0.5),
        "od_w_out": nrm(21, (O, MIX_WIDTH, D_MODEL), 0.5 * MIX_WIDTH ** -0.5),
        "rwkv_mu": unif(22, (O, RWKV_IN), 0.0, 1.0),
        "rwkv_w0": unif(23, (O, RWKV_WIDTH), -5.0, 1.0),
        "rwkv_w_up": nrm(24, (O, RWKV_DECAY_RANK, RWKV_WIDTH), 0.1),
        "rwkv_a0": nrm(25, (O, RWKV_WIDTH), 0.5),
        "rwkv_a_up": nrm(26, (O, RWKV_ICLR_RANK, RWKV_WIDTH), 0.1),
        "rwkv_g_up": nrm(27, (O, RWKV_GATE_RANK, RWKV_WIDTH), RWKV_GATE_RANK ** -0.5),
        "rwkv_k_k": 0.85 + nrm(28, (O, RWKV_WIDTH), 0.05),
        "rwkv_k_a": 1.0 + nrm(29, (O, RWKV_WIDTH), 0.05),
        "rwkv_r_k": nrm(30, (O, RWKV_HEADS, RWKV_HEAD_DIM), 0.1),
        "rwkv_ln_g": 1.0 + nrm(31, (O, RWKV_WIDTH), 0.05),
        "rwkv_ln_b": nrm(32, (O, RWKV_WIDTH), 0.02),
        "lru_conv_w": nrm(33, (O, LRU_CONV, LRU_WIDTH), 0.5),
        "lru_conv_b": nrm(34, (O, LRU_WIDTH), 0.02),
        "lru_wa": nrm(35, (O, LRU_BLOCKS, LRU_BLOCK_DIM, LRU_BLOCK_DIM), LRU_BLOCK_DIM ** -0.5),
        "lru_ba": nrm(36, (O, LRU_WIDTH), 0.02),
        "lru_wx": nrm(37, (O, LRU_BLOCKS, LRU_BLOCK_DIM, LRU_BLOCK_DIM), LRU_BLOCK_DIM ** -0.5),
        "lru_bx": nrm(38, (O, LRU_WIDTH), 0.02),
        "lru_lambda": jnp.log(lru_a) - jnp.log1p(-lru_a),
    }


def reference(x, mix_norm_g, ffn_norm_g, ffn_w_gate, ffn_w_up, ffn_w_down, final_norm_g,
              ev_w_in, ev_w_out, pool_w, pool_scale, ssd_conv_w, ssd_conv_b, ssd_dt_bias, ssd_a_log,
              ssd_d, ssd_norm_g, od_w_in, od_w_out, rwkv_mu, rwkv_w0, rwkv_w_up, rwkv_a0, rwkv_a_up,
              rwkv_g_up, rwkv_k_k, rwkv_k_a, rwkv_r_k, rwkv_ln_g, rwkv_ln_b, lru_conv_w, lru_conv_b,
              lru_wa, lru_ba, lru_wx, lru_bx, lru_lambda):
    h = x
    for layer in range(DEPTH):
        i = layer // 2
        hn = rmsnorm(h, mix_norm_g[layer])
        if layer % 2 == 0:
            mix = even_mixer(hn, ev_w_in[i], ev_w_out[i], pool_w[i], pool_scale[i], ssd_conv_w[i],
                             ssd_conv_b[i], ssd_dt_bias[i], ssd_a_log[i], ssd_d[i], ssd_norm_g[i])
        else:
            mix = odd_mixer(hn, od_w_in[i], od_w_out[i], rwkv_mu[i], rwkv_w0[i], rwkv_w_up[i], rwkv_a0[i],
                            rwkv_a_up[i], rwkv_g_up[i], rwkv_k_k[i], rwkv_k_a[i], rwkv_r_k[i], rwkv_ln_g[i],
                            rwkv_ln_b[i], lru_conv_w[i], lru_conv_b[i], lru_wa[i], lru_ba[i], lru_wx[i],
                            lru_bx[i], lru_lambda[i])
        h = h + mix
        hn = rmsnorm(h, ffn_norm_g[layer])
        h = h + swiglu(hn, ffn_w_gate[layer], ffn_w_up[layer], ffn_w_down[layer])
    return rmsnorm(h, final_norm_g)
```

```python
import contextlib
import numpy as np
import concourse.bass as bass
import concourse.mybir as mybir
from concourse.bass_utils import run_bass_kernel_spmd

F32 = mybir.dt.float32
BF16 = mybir.dt.bfloat16
ALU = mybir.AluOpType
AF = mybir.ActivationFunctionType
AX = mybir.AxisListType

D = 1024
SEQ = 16384
BATCH = 2
DEPTH = 4
DFF = 2816
EVEN_IN = 2316
ODD_IN = 2816
NCORES = 8
TOK = BATCH * SEQ // NCORES
RMS_EPS = 1e-6

SAME_ENGINE_SYNC = True


class Tile:
    def __init__(self, h, name):
        self.h = h
        self.name = name
        self.st = {}

    def __getitem__(self, idx):
        return self.h[idx]


class Op:
    __slots__ = ("eng", "fn", "reads", "writes", "deps", "need", "sig", "is_dma", "idx", "pre")


ENG_NAMES = ("pe", "act", "dve", "pool", "sp")


class Prog:
    def __init__(self):
        self.nc = bass.Bass("TRN2", target_bir_lowering=False)
        self.ops = []
        self.stack = contextlib.ExitStack()
        self.n_dma_sems = 8

    def dram(self, name, shape, dtype=F32, kind="ExternalInput"):
        return self.nc.dram_tensor(name, list(shape), dtype, kind=kind).ap()

    def sb(self, name, shape, dtype=F32):
        h = self.stack.enter_context(self.nc.sbuf_tensor(name, list(shape), dtype))
        return Tile(h, name)

    def ps(self, name, shape=(128, 512), dtype=F32):
        h = self.stack.enter_context(self.nc.psum_tensor(name, list(shape), dtype))
        return Tile(h, name)

    def op(self, eng, fn, reads=(), writes=(), dma=False):
        o = Op()
        o.eng = eng
        o.fn = fn
        o.reads = [r if isinstance(r, tuple) else (r, None) for r in reads]
        o.writes = [w if isinstance(w, tuple) else (w, None) for w in writes]
        o.is_dma = dma
        o.need = dma
        o.sig = None
        o.idx = len(self.ops)
        self.ops.append(o)
        return o

    def dma(self, out_ap, in_ap, reads=(), writes=(), eng="sp"):
        return self.op(eng, lambda e: e.dma_start(out=out_ap, in_=in_ap), reads, writes, dma=True)

    def mm(self, out_ap, lhsT, rhs, start, stop, reads=(), writes=()):
        return self.op("pe", lambda e: e.matmul(out_ap, lhsT, rhs, start=start, stop=stop), reads, writes)

    def transpose(self, out_ap, in_ap, ident_ap, reads=(), writes=()):
        return self.op("pe", lambda e: e.transpose(out_ap, in_ap, ident_ap), reads, writes)

    def act(self, out_ap, in_ap, func, bias=None, scale=None, reads=(), writes=(), accum_out=None):
        kw = {}
        if bias is not None:
            kw["bias"] = bias
        if scale is not None:
            kw["scale"] = scale
        if accum_out is not None:
            kw["accum_out"] = accum_out
        return self.op("act", lambda e: e.activation(out_ap, in_ap, func, **kw), reads, writes)

    def tt(self, out_ap, a, b, op, reads=(), writes=(), eng="dve"):
        return self.op(eng, lambda e: e.tensor_tensor(out_ap, a, b, op), reads, writes)

    def ts(self, out_ap, a, s1, s2, op0, op1=None, reads=(), writes=(), eng="dve"):
        if op1 is None:
            return self.op(eng, lambda e: e.tensor_scalar(out_ap, a, s1, 0.0, op0, ALU.add), reads, writes)
        return self.op(eng, lambda e: e.tensor_scalar(out_ap, a, s1, s2, op0, op1), reads, writes)

    def stt(self, out_ap, a, s, b, op0, op1, reads=(), writes=(), eng="dve"):
        return self.op(eng, lambda e: e.scalar_tensor_tensor(out_ap, a, s, b, op0, op1), reads, writes)

    def copy(self, out_ap, in_ap, reads=(), writes=(), eng="dve"):
        if eng == "act":
            return self.op("act", lambda e: e.copy(out_ap, in_ap), reads, writes)
        return self.op(eng, lambda e: e.tensor_copy(out_ap, in_ap), reads, writes)

    def memset(self, out_ap, val, writes=(), eng="dve"):
        return self.op(eng, lambda e: e.memset(out_ap, val), (), writes)

    def _analyse(self):
        for o in self.ops:
            deps = set()
            for (t, k) in o.reads:
                keys = list(t.st.keys()) if k is None else [kk for kk in (k, None) if kk in t.st]
                for kk in keys:
                    lw = t.st[kk][0]
                    if lw is not None:
                        deps.add(lw)
            for (t, k) in o.writes:
                keys = list(t.st.keys()) if k is None else [kk for kk in (k, None) if kk in t.st]
                for kk in keys:
                    lw, rd = t.st[kk]
                    if lw is not None:
                        deps.add(lw)
                    deps.update(rd)
            for (t, k) in o.reads:
                t.st.setdefault(k, [None, []])[1].append(o.idx)
            for (t, k) in o.writes:
                if k is None:
                    t.st = {None: [o.idx, []]}
                else:
                    t.st[k] = [o.idx, []]
            deps.discard(o.idx)
            fin = []
            for d in deps:
                p = self.ops[d]
                if p.eng == o.eng and not p.is_dma and not o.is_dma:
                    if p.eng == "pe" or not SAME_ENGINE_SYNC:
                        continue
                p.need = True
                fin.append(d)
            o.deps = fin

    def build(self):
        nc = self.nc
        self._analyse()
        st = self.stack
        sems = {e: st.enter_context(nc.semaphore("s_" + e)) for e in ENG_NAMES}
        dma_sems = {e: [st.enter_context(nc.semaphore("d_%s%d" % (e, i))) for i in range(self.n_dma_sems)]
                    for e in ("sp", "pool", "act")}
        cnt = {e: 0 for e in ENG_NAMES}
        dcnt = {e: [0] * self.n_dma_sems for e in dma_sems}
        drr = {e: 0 for e in dma_sems}
        per_eng = {e: [] for e in ENG_NAMES}
        for o in self.ops:
            o.pre = None
            if o.is_dma:
                j = drr[o.eng]
                drr[o.eng] = (j + 1) % self.n_dma_sems
                prev = dcnt[o.eng][j]
                dcnt[o.eng][j] = prev + 16
                o.sig = (dma_sems[o.eng][j], prev + 16, 16)
                o.pre = (dma_sems[o.eng][j], prev)
            elif o.need:
                cnt[o.eng] += 1
                o.sig = (sems[o.eng], cnt[o.eng], 1)
            per_eng[o.eng].append(o)
        self.final_waits = []
        with nc.Block() as block:
            def emit(eng_name, e):
                known = {}
                for o in per_eng[eng_name]:
                    waits = {}
                    for d in o.deps:
                        s, v, _ = self.ops[d].sig
                        if waits.get(id(s), (None, -1))[1] < v:
                            waits[id(s)] = (s, v)
                    if o.pre is not None and o.pre[1] > 0:
                        s, v = o.pre
                        if waits.get(id(s), (None, -1))[1] < v:
                            waits[id(s)] = (s, v)
                    for key, (s, v) in waits.items():
                        if known.get(key, 0) >= v:
                            continue
                        e.wait_ge(s, v)
                        known[key] = v
                    ins = o.fn(e)
                    if o.sig is not None:
                        ins.then_inc(o.sig[0], o.sig[2])
                if eng_name == "sp":
                    for q in dma_sems:
                        for j, s in enumerate(dma_sems[q]):
                            if dcnt[q][j] > 0:
                                e.wait_ge(s, dcnt[q][j])

            @block.tensor
            def _(e):
                emit("pe", e)

            @block.scalar
            def _(e):
                emit("act", e)

            @block.vector
            def _(e):
                emit("dve", e)

            @block.gpsimd
            def _(e):
                emit("pool", e)

            @block.sync
            def _(e):
                emit("sp", e)
        self.stack.close()
        return nc


class Ctx:
    def __init__(self, P):
        self.P = P
        self.rr = 0
        self.consts = {}
        self.ones_bf = P.sb("ones_bf", [128, 128], BF16)
        P.memset(self.ones_bf[:, :], 1.0, writes=[self.ones_bf])
        self.psum = [P.ps("psum%d" % i) for i in range(8)]
        self.pi = 0

    def cst(self, val):
        if val not in self.consts:
            t = self.P.sb("cst%d" % len(self.consts), [128, 1])
            self.P.memset(t[:, :], float(val), writes=[t])
            self.consts[val] = t
        return self.consts[val][:, 0:1]

    def next_ps(self):
        t = self.psum[self.pi]
        self.pi = (self.pi + 1) % 8
        return t

    def ve(self):
        self.rr += 1
        return "dve" if self.rr % 3 else "pool"


def load_weight(P, C, w_dram, K, N, name, stage_tiles):
    kc = K // 128
    wt = P.sb(name, [128, kc, N], BF16)
    i = 0
    for k in range(kc):
        c0 = 0
        while c0 < N:
            stg = stage_tiles[i % len(stage_tiles)]
            i += 1
            sv = stg.h[:, :, :].rearrange("p a b -> p (a b)") if len(stg.h.shape) == 3 else stg.h[:, :]
            cap = sv.shape[1]
            n = min(cap, N - c0)
            P.dma(sv[:, 0:n], w_dram[k * 128:(k + 1) * 128, c0:c0 + n], writes=[stg])
            half = n // 2
            P.copy(wt[:, k, c0:c0 + half], sv[:, 0:half], reads=[stg], writes=[(wt, (k, c0, 0))], eng="dve")
            P.copy(wt[:, k, c0 + half:c0 + n], sv[:, half:n], reads=[stg], writes=[(wt, (k, c0, 1))], eng="act")
            c0 += n
    return wt


def load_small(P, dram_ap, name, shape, dtype=F32):
    t = P.sb(name, shape, dtype)
    P.dma(t[:, :], dram_ap, writes=[t])
    return t


def rms_stats(P, C, src, kchunks, NT, sq, rstd, denom, eps):
    ps = C.next_ps()
    for i, k in enumerate(kchunks):
        P.tt(sq[:, k, :], src[:, k, :], src[:, k, :], ALU.mult, reads=[(src, k)], writes=[(sq, k)], eng="pool")
    for i, k in enumerate(kchunks):
        P.mm(ps[:, 0:NT], C.ones_bf[:, :], sq[:, k, :], i == 0, i == len(kchunks) - 1,
             reads=[C.ones_bf, (sq, k)], writes=[ps])
    P.act(rstd[:, :], ps[:, 0:NT], AF.Ln, bias=C.cst(eps), scale=1.0 / denom, reads=[ps], writes=[rstd])
    P.act(rstd[:, :], rstd[:, :], AF.Exp, scale=-0.5, reads=[rstd], writes=[rstd])


def rmsnorm_apply(P, C, src, g, kchunks, rstd, dst, gcol0=0):
    for k in kchunks:
        P.stt(dst[:, k, :], src[:, k, :], g[:, gcol0 + k:gcol0 + k + 1], rstd[:, :], ALU.mult, ALU.mult,
              reads=[(src, k), g, rstd], writes=[(dst, k)])


def proj(P, C, w, kc, rhs, NT, n_out, evac):
    noc = (n_out + 127) // 128
    for oc in range(noc):
        M = min(128, n_out - oc * 128)
        ps = C.next_ps()
        for k in range(kc):
            P.mm(ps[0:M, 0:NT], w[:, k, oc * 128:oc * 128 + M], rhs[:, k, :], k == 0, k == kc - 1,
                 reads=[w, (rhs, k)], writes=[ps])
        evac(oc, M, ps)


def fm(ap, t0, NT):
    return ap.rearrange("(c p) t -> p c t", p=128)[:, :, t0:t0 + NT]


def in_proj_tile(P, C, hn, w_in, F, NT, pT, t0, obufs):
    def evac(oc, M, ps):
        ob = obufs[oc % len(obufs)]
        if oc % 2 == 0:
            P.copy(ob[0:M, 0:NT], ps[0:M, 0:NT], reads=[ps], writes=[ob], eng="act")
        else:
            P.copy(ob[0:M, 0:NT], ps[0:M, 0:NT], reads=[ps], writes=[ob], eng="dve")
        P.dma(pT[oc * 128:oc * 128 + M, t0:t0 + NT], ob[0:M, 0:NT], reads=[ob])
    proj(P, C, w_in, 8, hn, NT, F, evac)


def build_A(F, ntok=TOK, NT=512):
    P = Prog()
    hT = P.dram("hT", [D, ntok])
    g_d = P.dram("g", [128, 8])
    w_d = P.dram("w_in", [D, F])
    pT = P.dram("pT", [F, ntok], kind="ExternalOutput")
    C = Ctx(P)
    hb = [P.sb("h%d" % i, [128, 8, NT]) for i in range(2)]
    w_in = load_weight(P, C, w_d, D, F, "w_in_sb", hb)
    g = load_small(P, g_d, "g_sb", [128, 8])
    sq = P.sb("sq", [128, 8, NT], BF16)
    hn = P.sb("hn", [128, 8, NT], BF16)
    rstd = P.sb("rstd", [128, NT])
    obufs = [P.sb("ob%d" % i, [128, NT]) for i in range(4)]
    for ti in range(ntok // NT):
        t0 = ti * NT
        h = hb[ti % 2]
        P.dma(h[:, :, :], fm(hT, t0, NT), writes=[h])
        rms_stats(P, C, h, range(8), NT, sq, rstd, float(D), RMS_EPS)
        rmsnorm_apply(P, C, h, g, range(8), rstd, hn)
        in_proj_tile(P, C, hn, w_in, F, NT, pT, t0, obufs)
    return P.build()


def build_C1(even, ntok=TOK, NT=256):
    P = Prog()
    hT = P.dram("hT", [D, ntok])
    yT = P.dram("yT", [D, ntok])
    g_d = P.dram("g", [128, 8])
    wo_d = P.dram("w_out", [D, D])
    wg_d = P.dram("w_gate", [D, DFF])
    wu_d = P.dram("w_up", [D, DFF])
    if even:
        sg_d = P.dram("ssd_g", [128, 8])
    h1T = P.dram("h1T", [D, ntok], kind="ExternalOutput")
    actT = P.dram("actT", [DFF, ntok], BF16, kind="ExternalOutput")
    C = Ctx(P)
    hb = [P.sb("h%d" % i, [128, 8, NT]) for i in range(2)]
    yb = [P.sb("y%d" % i, [128, 8, NT]) for i in range(2)]
    stage = hb + yb
    w_out = load_weight(P, C, wo_d, D, D, "w_out_sb", stage)
    w_gate = load_weight(P, C, wg_d, D, DFF, "w_gate_sb", stage)
    w_up = load_weight(P, C, wu_d, D, DFF, "w_up_sb", stage)
    g = load_small(P, g_d, "g_sb", [128, 8])
    if even:
        sg = load_small(P, sg_d, "sg_sb", [128, 8])
    sq = P.sb("sq", [128, 8, NT], BF16)
    ybf = P.sb("ybf", [128, 8, NT], BF16)
    hn = P.sb("hn", [128, 8, NT], BF16)
    rstd = P.sb("rstd", [128, NT])
    rstdg = [P.sb("rstdg%d" % i, [128, NT]) for i in range(2)]
    tmps = [P.sb("tmp%d" % i, [128, NT]) for i in range(2)]
    obufs = [P.sb("ob%d" % i, [128, NT], BF16) for i in range(4)]
    for ti in range(ntok // NT):
        t0 = ti * NT
        h = hb[ti % 2]
        y = yb[ti % 2]
        P.dma(h[:, :, :], fm(hT, t0, NT), writes=[h])
        P.dma(y[:, :, :], fm(yT, t0, NT), writes=[y], eng="pool")
        if even:
            for gi, grp in enumerate(((2, 3, 4), (5, 6, 7))):
                rms_stats(P, C, y, grp, NT, sq, rstdg[gi], 384.0, RMS_EPS)
                rmsnorm_apply(P, C, y, sg, grp, rstdg[gi], ybf)
            for k in (0, 1):
                P.copy(ybf[:, k, :], y[:, k, :], reads=[(y, k)], writes=[(ybf, k)], eng="pool")
        else:
            for k in range(8):
                P.copy(ybf[:, k, :], y[:, k, :], reads=[(y, k)], writes=[(ybf, k)], eng=("pool", "dve")[k % 2])

        def evac_mix(oc, M, ps):
            P.tt(h[:, oc, :], h[:, oc, :], ps[:, 0:NT], ALU.add, reads=[(h, oc), ps], writes=[(h, oc)])
        proj(P, C, w_out, 8, ybf, NT, D, evac_mix)
        P.dma(fm(h1T, t0, NT), h[:, :, :], reads=[h])
        rms_stats(P, C, h, range(8), NT, sq, rstd, float(D), RMS_EPS)
        rmsnorm_apply(P, C, h, g, range(8), rstd, hn)
        for fc in range(DFF // 128):
            psA = C.next_ps()
            psB = C.next_ps()
            for k in range(8):
                P.mm(psA[:, 0:NT], w_gate[:, k, fc * 128:(fc + 1) * 128], hn[:, k, :], k == 0, k == 7,
                     reads=[w_gate, (hn, k)], writes=[psA])
            for k in range(8):
                P.mm(psB[:, 0:NT], w_up[:, k, fc * 128:(fc + 1) * 128], hn[:, k, :], k == 0, k == 7,
                     reads=[w_up, (hn, k)], writes=[psB])
            tmp = tmps[fc % 2]
            ob = obufs[fc % 4]
            P.act(tmp[:, :], psA[:, 0:NT], AF.Silu, reads=[psA], writes=[tmp])
            P.tt(ob[:, :], tmp[:, :], psB[:, 0:NT], ALU.mult, reads=[tmp, psB], writes=[ob])
            P.dma(actT[fc * 128:(fc + 1) * 128, t0:t0 + NT], ob[:, :], reads=[ob], eng=("sp", "pool")[fc % 2])
    return P.build()


def build_C2(F, final, ntok=TOK, NT=256):
    P = Prog()
    hT = P.dram("hT", [D, ntok])
    actT = P.dram("actT", [DFF, ntok], BF16)
    g_d = P.dram("g", [128, 8])
    wd_d = P.dram("w_down", [DFF, D])
    if final:
        outT = P.dram("outT", [D, ntok], kind="ExternalOutput")
    else:
        w_d = P.dram("w_in", [D, F])
        h2T = P.dram("h2T", [D, ntok], kind="ExternalOutput")
        pT = P.dram("pT", [F, ntok], kind="ExternalOutput")
    C = Ctx(P)
    hb = [P.sb("h%d" % i, [128, 8, NT]) for i in range(2)]
    ab = [P.sb("a%d" % i, [128, DFF // 128, NT], BF16) for i in range(2)]
    w_down = load_weight(P, C, wd_d, DFF, D, "w_down_sb", hb)
    if not final:
        w_in = load_weight(P, C, w_d, D, F, "w_in_sb", hb)
    g = load_small(P, g_d, "g_sb", [128, 8])
    sq = P.sb("sq", [128, 8, NT], BF16)
    hn = P.sb("hn", [128, 8, NT], BF16)
    rstd = P.sb("rstd", [128, NT])
    obufs = [P.sb("ob%d" % i, [128, NT]) for i in range(4)]
    if final:
        of = P.sb("of", [128, 8, NT])
    for ti in range(ntok // NT):
        t0 = ti * NT
        h = hb[ti % 2]
        a = ab[ti % 2]
        P.dma(h[:, :, :], fm(hT, t0, NT), writes=[h])
        P.dma(a[:, :, :], fm(actT, t0, NT), writes=[a], eng="pool")

        def evac_dn(oc, M, ps):
            P.tt(h[:, oc, :], h[:, oc, :], ps[:, 0:NT], ALU.add, reads=[(h, oc), ps], writes=[(h, oc)])
        proj(P, C, w_down, DFF // 128, a, NT, D, evac_dn)
        rms_stats(P, C, h, range(8), NT, sq, rstd, float(D), RMS_EPS)
        if final:
            rmsnorm_apply(P, C, h, g, range(8), rstd, of)
            P.dma(fm(outT, t0, NT), of[:, :, :], reads=[of])
        else:
            P.dma(fm(h2T, t0, NT), h[:, :, :], reads=[h])
            rmsnorm_apply(P, C, h, g, range(8), rstd, hn)
            in_proj_tile(P, C, hn, w_in, F, NT, pT, t0, obufs)
    return P.build()


def bc3(ap2, n):
    return ap2.unsqueeze(2).broadcast_to([ap2.shape[0], ap2.shape[1], n])


def build_Meven(S=SEQ, N=512):
    P = Prog()
    ptok = S // 4
    pin = P.dram("pin", [643, S])
    upad = P.dram("upad", [256, 16 + ptok])
    pinv_d = P.dram("pinv", [128, 2, N])
    poolw_d = P.dram("poolw", [128, 2, 64])
    pscale_d = P.dram("pscale", [64, 4])
    convw_d = P.dram("convw", [128, 4, 4])
    convb_d = P.dram("convb", [128, 4])
    dtp_d = P.dram("dtp", [3, 2])
    dvec_d = P.dram("dvec", [128, 2])
    cst_d = P.dram("cst", [128, 4, 128])
    ypool = P.dram("ypool", [256, ptok], kind="ExternalOutput")
    yssd = P.dram("yssd", [192, S], kind="ExternalOutput")
    C = Ctx(P)
    cst = P.sb("cst_sb", [128, 4, 128])
    P.dma(cst[:, :, :], cst_d, writes=[cst])
    ident, triU, maskLT, negI = (cst[:, i, :] for i in range(4))
    ones32 = P.sb("ones32", [128, 128])
    P.memset(ones32[:, :], 1.0, writes=[ones32])
    convw = load_small(P, convw_d.rearrange("p a b -> p (a b)"), "convw_sb", [128, 16])
    convb = load_small(P, convb_d, "convb_sb", [128, 4])
    dtp = load_small(P, dtp_d, "dtp_sb", [3, 2])
    dvec = load_small(P, dvec_d, "dvec_sb", [128, 2])
    pscale = load_small(P, pscale_d, "pscale_sb", [64, 4])
    pinv = P.sb("pinv_sb", [128, 2, N])
    P.dma(pinv[:, :, :], pinv_d, writes=[pinv])
    pw32 = P.sb("pw32", [128, 2, 64])
    P.dma(pw32[:, :, :], poolw_d, writes=[pw32])
    pw = P.sb("pw_bf", [128, 2, 64], BF16)
    P.copy(pw[:, :, :], pw32[:, :, :], reads=[pw32], writes=[pw])
    ident_bf = P.sb("ident_bf", [128, 128], BF16)
    P.copy(ident_bf[:, :], ident, reads=[cst], writes=[ident_bf])
    aneg = P.sb("aneg", [3, 1])
    P.act(aneg[:, :], dtp[:, 1:2], AF.Exp, reads=[dtp], writes=[aneg])
    P.ts(aneg[:, :], aneg[:, :], -1.0, None, ALU.mult, reads=[aneg], writes=[aneg])

    pu = [P.sb("pu%d" % i, [128, 2, 16 + N]) for i in range(2)]
    pa = P.sb("pa", [128, 16 + N])
    pb = P.sb("pb", [128, 16 + N])
    pd = P.sb("pd", [128, N])
    pdb = P.sb("pdb", [128, N], BF16)
    pob = [P.sb("pob%d" % i, [64, N]) for i in range(2)]
    for ti in range(ptok // N):
        t0 = ti * N
        u = pu[ti % 2]
        P.dma(u[:, :, :], upad.rearrange("(a p) t -> p a t", p=128)[:, :, t0:t0 + 16 + N], writes=[u])
        for g in range(4):
            pt, ph = g // 2, g % 2
            sl = slice(ph * 64, ph * 64 + 64)
            cur = u[sl, pt, :]
            curT = (u, None)
            bufs = [pa, pb]
            for j in range(g + 1):
                sh = 1 << j
                nxt = bufs[j % 2]
                P.tt(nxt[sl, sh:16 + N], cur[:, sh:16 + N], cur[:, 0:16 + N - sh], ALU.add,
                     reads=[curT], writes=[(nxt, g)], eng="pool")
                cur = nxt[sl, :]
                curT = (nxt, g)
            w = float(1 << (g + 1))
            if ti == 0:
                P.tt(pd[sl, :], cur[:, 16:16 + N], pinv[sl, pt, :], ALU.mult, reads=[curT, pinv], writes=[(pd, g)])
                P.tt(pdb[sl, :], pd[sl, :], u[sl, pt, 16:16 + N], ALU.subtract, reads=[(pd, g), u], writes=[(pdb, g)])
            else:
                P.stt(pdb[sl, :], cur[:, 16:16 + N], 1.0 / w, u[sl, pt, 16:16 + N], ALU.mult, ALU.subtract,
                      reads=[curT, u], writes=[(pdb, g)])
            ps = C.next_ps()
            P.mm(ps[0:64, 0:N], pw[sl, pt, :], pdb[sl, :], True, True, reads=[pw, (pdb, g)], writes=[ps])
            ob = pob[g % 2]
            P.act(ob[:, :], ps[0:64, 0:N], AF.Copy, scale=pscale[:, g:g + 1], reads=[ps, pscale], writes=[ob])
            P.dma(ypool[g * 64:(g + 1) * 64, t0:t0 + N], ob[:, :], reads=[ob])

    NCH = N // 128
    names = ("z1", "z2", "x1", "x2", "B", "C")
    rows = {"z1": (0, 128), "z2": (128, 64), "x1": (192, 128), "x2": (320, 64), "B": (384, 128), "C": (512, 128)}
    raw = {n: [P.sb("r%s%d" % (n, i), [128, 3 + N]) for i in range(2)] for n in names}
    rdt = [P.sb("rdt%d" % i, [3, N]) for i in range(2)]
    acc = P.sb("cacc", [128, N])
    xs1 = P.sb("xs1", [128, N])
    xs2 = P.sb("xs2", [128, N])
    Bb = P.sb("Bb", [128, N], BF16)
    Cb = P.sb("Cb", [128, N], BF16)
    dd = P.sb("dd", [3, 2, N])
    dtmp = P.sb("dtmp", [3, 3, N])
    dT = P.sb("dT", [128, 6])
    cb = P.sb("cbuf", [128, 6])
    ex = P.sb("ex", [128, 9])
    w2 = P.sb("w2", [128, 3])
    lh = [P.sb("lh%d" % i, [128, 128]) for i in range(2)]
    decT = [P.sb("decT%d" % i, [128, 128]) for i in range(2)]
    MT = [P.sb("MT%d" % i, [128, 128], BF16) for i in range(3)]
    xdt = P.sb("xdt", [128, 192], BF16)
    xdte = P.sb("xdte", [128, 192], BF16)
    BT = P.sb("BT", [128, 128], BF16)
    ysb = P.sb("ysb", [128, 192])
    yf1 = P.sb("yf1", [128, N])
    yf2 = P.sb("yf2", [128, N])
    sz = P.sb("sz", [128, N])
    og1 = [P.sb("og1_%d" % i, [128, N]) for i in range(2)]
    og2 = [P.sb("og2_%d" % i, [128, N]) for i in range(2)]
    state = P.sb("state", [128, 192])
    stmp = P.sb("stmp", [128, 192])
    state_bf = P.sb("state_bf", [128, 192], BF16)
    P.memset(state[:, :], 0.0, writes=[state])
    P.memset(state_bf[:, :], 0.0, writes=[state_bf])
    ps_bT = P.stack.enter_context(P.nc.psum_tensor("ps_bT", [128, 1024], BF16)) if False else None
    PSD, PSG, PSS, PSX, PSY, PSO, PST, PSW = C.psum

    for ti in range(S // N):
        t0 = ti * N
        cur = {}
        for n in names:
            r0, nr = rows[n]
            t = raw[n][ti % 2]
            if ti == 0:
                P.memset(t[:, 0:3], 0.0, writes=[t])
                P.dma(t[0:nr, 3:3 + N], pin[r0:r0 + nr, 0:N], writes=[t], eng="pool")
            else:
                P.dma(t[0:nr, 0:3 + N], pin[r0:r0 + nr, t0 - 3:t0 + N], writes=[t], eng="pool")
            cur[n] = t
        rd = rdt[ti % 2]
        P.dma(rd[:, :], pin[640:643, t0:t0 + N], writes=[rd], eng="pool")
        for ci, (n, dst, nr) in enumerate((("x1", xs1, 128), ("x2", xs2, 64), ("B", Bb, 128), ("C", Cb, 128))):
            t = cur[n]
            P.ts(acc[0:nr, :], t[0:nr, 0:N], convw[0:nr, ci * 4:ci * 4 + 1], None, ALU.mult, reads=[t, convw], writes=[acc])
            for k in (1, 2, 3):
                P.stt(acc[0:nr, :], t[0:nr, k:k + N], convw[0:nr, ci * 4 + k:ci * 4 + k + 1], acc[0:nr, :], ALU.mult, ALU.add,
                      reads=[t, convw, acc], writes=[acc])
            P.act(dst[0:nr, :], acc[0:nr, :], AF.Silu, bias=convb[0:nr, ci:ci + 1], reads=[acc, convb], writes=[dst])
        P.ts(dtmp[:, 0, :], rd[:, :], dtp[:, 0:1], None, ALU.add, reads=[rd, dtp], writes=[(dtmp, 0)])
        P.ts(dtmp[:, 1, :], dtmp[:, 0, :], -1.0, None, ALU.mult, reads=[(dtmp, 0)], writes=[(dtmp, 1)])
        P.tt(dtmp[:, 1, :], dtmp[:, 1, :], dtmp[:, 0, :], ALU.max, reads=[(dtmp, 0), (dtmp, 1)], writes=[(dtmp, 1)])
        P.act(dtmp[:, 1, :], dtmp[:, 1, :], AF.Exp, scale=-1.0, reads=[(dtmp, 1)], writes=[(dtmp, 1)])
        P.act(dtmp[:, 1, :], dtmp[:, 1, :], AF.Ln, bias=C.cst(1.0)[0:3, :], reads=[(dtmp, 1)], writes=[(dtmp, 1)])
        P.ts(dtmp[:, 2, :], dtmp[:, 0, :], 0.0, None, ALU.max, reads=[(dtmp, 0)], writes=[(dtmp, 2)])
        P.tt(dd[:, 0, :], dtmp[:, 2, :], dtmp[:, 1, :], ALU.add, reads=[(dtmp, 1), (dtmp, 2)], writes=[(dd, 0)])
        P.ts(dd[:, 1, :], dd[:, 0, :], aneg[:, 0:1], None, ALU.mult, reads=[(dd, 0), aneg], writes=[(dd, 1)])
        for c in range(NCH):
            cs = slice(c * 128, (c + 1) * 128)
            P.mm(PSD[:, 0:3], dd[:, 0, cs], ident[0:3, 0:3], True, True, reads=[(dd, 0), cst], writes=[PSD])
            P.mm(PSD[:, 3:6], dd[:, 1, cs], ident[0:3, 0:3], True, True, reads=[(dd, 1), cst], writes=[PSD])
            P.copy(dT[:, :], PSD[:, 0:6], reads=[PSD], writes=[dT])
            P.mm(PSD[:, 8:11], triU, dT[:, 3:6], True, True, reads=[cst, dT], writes=[PSD])
            P.mm(PSD[:, 11:14], ones32[:, :], dT[:, 3:6], True, True, reads=[ones32, dT], writes=[PSD])
            P.copy(cb[:, :], PSD[:, 8:14], reads=[PSD], writes=[cb])
            P.act(ex[:, 0:3], cb[:, 0:3], AF.Exp, reads=[cb], writes=[(ex, 0)])
            P.tt(ex[:, 3:6], cb[:, 3:6], cb[:, 0:3], ALU.subtract, reads=[cb], writes=[(ex, 1)])
            P.act(ex[:, 3:9], ex[:, 3:9], AF.Exp, reads=[(ex, 1)], writes=[(ex, 1)]) if False else None
            P.act(ex[:, 3:6], ex[:, 3:6], AF.Exp, reads=[(ex, 1)], writes=[(ex, 1)])
            P.act(ex[:, 6:9], cb[:, 3:6], AF.Exp, reads=[cb], writes=[(ex, 2)])
            P.tt(w2[:, :], dT[:, 0:3], ex[:, 3:6], ALU.mult, reads=[dT, (ex, 1)], writes=[w2])
            P.mm(PSG[:, 0:128], Bb[:, cs], Cb[:, cs], True, True, reads=[Bb, Cb], writes=[PSG])
            P.transpose(PSX[:, 0:128], xs1[:, cs], ident, reads=[xs1, cst], writes=[PSX])
            P.transpose(PSX[:, 128:192], xs2[0:64, cs], ident[0:64, 0:64], reads=[xs2, cst], writes=[PSX])
            px3 = PSX[:, 0:192].rearrange("p (h d) -> p h d", h=3)
            P.tt(xdt[:, :].rearrange("p (h d) -> p h d", h=3), px3, bc3(dT[:, 0:3], 64), ALU.mult,
                 reads=[PSX, dT], writes=[xdt])
            P.tt(xdte[:, :].rearrange("p (h d) -> p h d", h=3), px3, bc3(w2[:, 0:3], 64), ALU.mult,
                 reads=[PSX, w2], writes=[xdte])
            P.mm(PSW[:, 0:128], Bb[:, cs], ident_bf[:, :], True, True, reads=[Bb, ident_bf], writes=[PSW])
            P.copy(BT[:, :], PSW[:, 0:128], reads=[PSW], writes=[BT], eng="act")
            for hh in range(3):
                l_ = lh[hh % 2]
                d_ = decT[hh % 2]
                m_ = MT[hh]
                P.ts(l_[:, :], maskLT, dT[:, 3 + hh:4 + hh], None, ALU.mult, reads=[cst, dT], writes=[l_], eng="pool")
                P.mm(PSS[:, 0:128], l_[:, :], triU, True, False, reads=[l_, cst], writes=[PSS])
                P.mm(PSS[:, 0:128], negI, maskLT, False, True, reads=[cst], writes=[PSS])
                P.act(d_[:, :], PSS[:, 0:128], AF.Exp, reads=[PSS], writes=[d_])
                P.tt(m_[:, :], d_[:, :], PSG[:, 0:128], ALU.mult, reads=[d_, PSG], writes=[m_])
                P.mm(PSY[:, hh * 64:(hh + 1) * 64], m_[:, :], xdt[:, hh * 64:(hh + 1) * 64], True, True,
                     reads=[m_, xdt], writes=[PSY])
            P.mm(PSO[:, 0:192], Cb[:, cs], state_bf[:, :], True, True, reads=[Cb, state_bf], writes=[PSO])
            P.tt(ysb[:, :].rearrange("p (h d) -> p h d", h=3), PSO[:, 0:192].rearrange("p (h d) -> p h d", h=3),
                 bc3(ex[:, 0:3], 64), ALU.mult, reads=[PSO, (ex, 0)], writes=[ysb])
            P.tt(ysb[:, :], ysb[:, :], PSY[:, 0:192], ALU.add, reads=[ysb, PSY], writes=[ysb])
            P.transpose(PST[:, 0:128], ysb[:, 0:128], ident, reads=[ysb, cst], writes=[PST])
            P.transpose(PST[0:64, 128:256], ysb[:, 128:192], ident, reads=[ysb, cst], writes=[PST])
            P.copy(yf1[:, cs], PST[:, 0:128], reads=[PST], writes=[(yf1, c)], eng="act")
            P.copy(yf2[0:64, cs], PST[0:64, 128:256], reads=[PST], writes=[(yf2, c)], eng="act")
            P.mm(PSW[:, 128:320], BT[:, :], xdte[:, :], True, True, reads=[BT, xdte], writes=[PSW])
            P.tt(stmp[:, :].rearrange("p (h d) -> p h d", h=3), state[:, :].rearrange("p (h d) -> p h d", h=3),
                 bc3(ex[:, 6:9], 64), ALU.mult, reads=[state, (ex, 2)], writes=[stmp])
            P.tt(state[:, :], stmp[:, :], PSW[:, 128:320], ALU.add, reads=[stmp, PSW], writes=[state])
            P.copy(state_bf[:, :], state[:, :], reads=[state], writes=[state_bf], eng="pool")
        for (yf, xs, zt, og, nr, r0, dc) in ((yf1, xs1, cur["z1"], og1[ti % 2], 128, 0, 0),
                                             (yf2, xs2, cur["z2"], og2[ti % 2], 64, 128, 1)):
            P.stt(yf[0:nr, :], xs[0:nr, :], dvec[0:nr, dc:dc + 1], yf[0:nr, :], ALU.mult, ALU.add,
                  reads=[xs, dvec, yf], writes=[yf])
            P.act(sz[0:nr, :], zt[0:nr, 3:3 + N], AF.Silu, reads=[zt], writes=[sz])
            P.tt(og[0:nr, :], yf[0:nr, :], sz[0:nr, :], ALU.mult, reads=[yf, sz], writes=[og], eng="pool")
            P.dma(yssd[r0:r0 + nr, t0:t0 + N], og[0:nr, :], reads=[og])
    return P.build()


def _consts():
    j = np.arange(128)[:, None]
    l = np.arange(128)[None, :]
    c = np.zeros((128, 4, 128), np.float32)
    c[:, 0, :] = (j == l)
    c[:, 1, :] = (j <= l)
    c[:, 2, :] = (l < j)
    c[:, 3, :] = -30000.0 * (j == l)
    return c


def pad128(a):
    out = np.zeros((128,) + a.shape[1:], np.float32)
    out[:a.shape[0]] = a
    return out


def pack_even(pTb, q, S, prm, N=512):
    g = q // 2
    ptok = S // 4
    xo = 1024
    rows = [pTb[256 + 192 * q:256 + 192 * q + 192], pTb[xo + 192 * q:xo + 192 * q + 192],
            pTb[xo + 768 + 128 * g:xo + 768 + 128 * g + 128], pTb[xo + 1024 + 128 * g:xo + 1024 + 128 * g + 128],
            pTb[2304 + 3 * q:2304 + 3 * q + 3]]
    pin = np.ascontiguousarray(np.concatenate(rows, 0))
    upad = np.zeros((256, 16 + ptok), np.float32)
    lo = q * ptok - 16
    if lo < 0:
        upad[:, 16:] = pTb[0:256, 0:ptok]
    else:
        upad[:, :] = pTb[0:256, lo:lo + 16 + ptok]
    t = np.arange(N)
    pinv = np.zeros((128, 2, N), np.float32)
    for gg in range(4):
        w = 2 << gg
        cnt = np.minimum(t + 1, w) if q == 0 else np.full(N, w)
        pinv[(gg % 2) * 64:(gg % 2) * 64 + 64, gg // 2, :] = (1.0 / cnt.astype(np.float64)).astype(np.float32)[None, :]
    poolw = np.zeros((128, 2, 64), np.float32)
    for gg in range(4):
        poolw[(gg % 2) * 64:(gg % 2) * 64 + 64, gg // 2, :] = prm["pool_w"][gg]
    pscale = np.ascontiguousarray(prm["pool_scale"].reshape(4, 64).T)
    cw = prm["ssd_conv_w"]
    cbias = prm["ssd_conv_b"]
    chans = [np.arange(192 * q, 192 * q + 128), np.arange(192 * q + 128, 192 * q + 192),
             np.arange(768 + 128 * g, 768 + 128 * g + 128), np.arange(1024 + 128 * g, 1024 + 128 * g + 128)]
    convw = np.zeros((128, 4, 4), np.float32)
    convb = np.zeros((128, 4), np.float32)
    for i, ch in enumerate(chans):
        convw[:len(ch), i, :] = cw[:, ch].T
        convb[:len(ch), i] = cbias[ch]
    dtp = np.stack([prm["ssd_dt_bias"][3 * q:3 * q + 3], prm["ssd_a_log"][3 * q:3 * q + 3]], 1).astype(np.float32)
    dch = np.repeat(prm["ssd_d"][3 * q:3 * q + 3], 64)
    dvec = np.zeros((128, 2), np.float32)
    dvec[:, 0] = dch[:128]
    dvec[:64, 1] = dch[128:]
    return {"pin": pin, "upad": upad, "pinv": pinv, "poolw": poolw, "pscale": pscale, "convw": convw,
            "convb": convb, "dtp": np.ascontiguousarray(dtp), "dvec": dvec, "cst": _consts()}


def build_Modd(S=SEQ, N=512):
    P = Prog()
    CH = 64
    NCH = N // CH
    pin = P.dram("pin", [896, S])
    prm_d = P.dram("prm", [128, 24])
    wup_d = P.dram("wup", [128, 128])
    gup_d = P.dram("gup", [128, 128])
    lw_d = P.dram("lruw", [128, 2, 128])
    lng_d = P.dram("lng", [128, 2, 64])
    cst_d = P.dram("cst", [128, 6, 128])
    rmask_d = P.dram("rmask", [128, N])
    yrw = P.dram("yrw", [S, 128], kind="ExternalOutput")
    ylru = P.dram("ylru", [128, S], kind="ExternalOutput")
    C = Ctx(P)
    cst = P.sb("cst_sb", [128, 6, 128])
    P.dma(cst[:, :, :], cst_d, writes=[cst])
    ident, m_su, m_sl, m_iu, bones, isel = (cst[:, i, :] for i in range(6))
    prm = load_small(P, prm_d, "prm_sb", [128, 24])
    rmask = load_small(P, rmask_d, "rmask_sb", [128, N])
    lng = P.sb("lng_sb", [128, 2, 64])
    P.dma(lng[:, :, :], lng_d, writes=[lng])
    wup32 = load_small(P, wup_d, "wup32", [128, 128])
    wup = P.sb("wup_bf", [128, 128], BF16)
    P.copy(wup[:, :], wup32[:, :], reads=[wup32], writes=[wup])
    gup32 = load_small(P, gup_d, "gup32", [128, 128])
    gup = P.sb("gup_bf", [128, 128], BF16)
    P.copy(gup[:, :], gup32[:, :], reads=[gup32], writes=[gup])
    lw32 = P.sb("lw32", [128, 2, 128])
    P.dma(lw32[:, :, :], lw_d, writes=[lw32])
    lwb = P.sb("lw_bf", [128, 2, 128], BF16)
    P.copy(lwb[:, :, :], lw32[:, :, :], reads=[lw32], writes=[lwb])
    ones1 = P.sb("ones1", [128, 1])
    P.memset(ones1[:, :], 1.0, writes=[ones1])
    MU, W0, A0, KK_, KA, RK_, LCB, LCW, BA, BX, LAM = 0, 6, 7, 8, 9, 10, 11, 12, 16, 17, 18
    omka = P.sb("omka", [128, 1])
    P.ts(omka[:, :], prm[:, KA:KA + 1], -1.0, 1.0, ALU.mult, ALU.add, reads=[prm], writes=[omka])
    lt = P.sb("lt", [128, 4])
    P.ts(lt[:, 0:1], prm[:, LAM:LAM + 1], -1.0, None, ALU.mult, reads=[prm], writes=[lt])
    P.tt(lt[:, 1:2], lt[:, 0:1], prm[:, LAM:LAM + 1], ALU.max, reads=[lt, prm], writes=[lt])
    P.act(lt[:, 1:2], lt[:, 1:2], AF.Exp, scale=-1.0, reads=[lt], writes=[lt])
    P.act(lt[:, 1:2], lt[:, 1:2], AF.Ln, bias=C.cst(1.0), reads=[lt], writes=[lt])
    P.ts(lt[:, 2:3], lt[:, 0:1], 0.0, None, ALU.max, reads=[lt], writes=[lt])
    P.tt(lt[:, 2:3], lt[:, 2:3], lt[:, 1:2], ALU.add, reads=[lt], writes=[lt])
    lc = P.sb("lc", [128, 2])
    P.ts(lc[:, 0:1], lt[:, 2:3], -8.0, None, ALU.mult, reads=[lt], writes=[lc])
    P.ts(lc[:, 1:2], lt[:, 2:3], -16.0, None, ALU.mult, reads=[lt], writes=[lc])

    names = ("r", "k", "v", "wa", "gd", "lg", "lx")
    rows = {"r": (0, 128), "k": (128, 128), "v": (256, 128), "wa": (384, 128), "gd": (512, 128),
            "lg": (640, 128), "lx": (768, 128)}
    halo = {"r": 1, "k": 1, "v": 1, "wa": 1, "gd": 1, "lg": 0, "lx": 3}
    raw = {n: [P.sb("r%s%d" % (n, i), [128, halo[n] + N]) for i in range(2)] for n in names}

    def T(name, dt=F32, shape=None):
        return P.sb(name, shape or [128, N], dt)
    tmp = T("tmp")
    r_ = T("r_"); k_ = T("k_"); v_ = T("v_"); wa_ = T("wa_"); gsh = T("gsh")
    twb = T("twb", BF16)
    sgp = P.sb("sgp", [128, NCH, 192], BF16)
    P.memset(sgp[:, :, :], 0.0, writes=[sgp])
    lw_ = T("lw_"); a_ = T("a_"); kk = T("kk"); kp = T("kp"); b_ = T("b_"); prod = T("prod")
    cl = T("cl"); Wq = T("Wq"); Wkk = T("Wkk"); Winv = T("Winv"); Wend = T("Wend")
    WC = P.sb("WC", [128, NCH])
    NB = 2
    bd = {n: [P.sb("bd_%s%d" % (n, i), [128, 128]) for i in range(NB)] for n in ("bt", "kt", "vb", "bh", "kh", "pr")}
    RKb = [P.sb("RK%d" % i, [128, 256]) for i in range(NB)]
    for n in bd:
        for t in bd[n]:
            P.memset(t[:, :], 0.0, writes=[t])
    for t in RKb:
        P.memset(t[:, :], 0.0, writes=[t])
    mask2 = P.sb("mask2", [128, 256])
    P.copy(mask2[:, 0:128], m_su, reads=[cst], writes=[mask2])
    P.copy(mask2[:, 128:256], m_iu, reads=[cst], writes=[mask2])
    M1s = [P.sb("M1_%d" % i, [128, 256]) for i in range(2)]
    M2s = [P.sb("M2_%d" % i, [128, 256]) for i in range(2)]
    Pw = [P.sb("Pw%d" % i, [128, 128]) for i in range(4)]
    PTw = [P.sb("PTw%d" % i, [128, 128]) for i in range(4)]
    Xs = [P.sb("Xs%d" % i, [128, 128]) for i in range(4)]
    VTs = [P.sb("VT%d" % i, [128, 64]) for i in range(2)]
    Rsbs = [P.sb("Rsb%d" % i, [128, 64]) for i in range(2)]
    SAs = [P.sb("SA%d" % i, [128, 64]) for i in range(2)]
    bhTs = [P.sb("bhT%d" % i, [128, 128]) for i in range(2)]
    khTs = [P.sb("khT%d" % i, [128, 128]) for i in range(2)]
    ST = P.sb("ST", [128, 64])
    P.memset(ST[:, :], 0.0, writes=[ST])
    ysb = P.sb("ysb", [128, 64]); junk = P.sb("junk", [128, 64]); yn = P.sb("yn", [128, 64])
    st4 = P.sb("st4", [128, 8])
    osb = [P.sb("osb%d" % i, [128, NCH, 64]) for i in range(2)]
    lacc = T("lacc"); lxc = T("lxc"); lxb = T("lxb", BF16); lrg = T("lrg"); lig = T("lig"); la = T("la")
    lb = T("lb"); lh = T("lh"); lgt = T("lgt")
    lout = [T("lout%d" % i) for i in range(2)]
    hst = P.sb("hst", [128, 1])
    P.memset(hst[:, :], 0.0, writes=[hst])
    PS = C.psum

    for ti in range(S // N):
        t0 = ti * N
        cur = {}
        for n in names:
            r0, nr = rows[n]
            hl = halo[n]
            t = raw[n][ti % 2]
            if ti == 0 and hl > 0:
                P.memset(t[:, 0:hl], 0.0, writes=[t])
                P.dma(t[:, hl:hl + N], pin[r0:r0 + nr, 0:N], writes=[t], eng="pool")
            else:
                P.dma(t[:, 0:hl + N], pin[r0:r0 + nr, t0 - hl:t0 + N], writes=[t], eng="pool")
            cur[n] = t
        t = cur["lx"]
        P.ts(lacc[:, :], t[:, 0:N], prm[:, LCW:LCW + 1], None, ALU.mult, reads=[t, prm], writes=[lacc])
        for k in (1, 2, 3):
            P.stt(lacc[:, :], t[:, k:k + N], prm[:, LCW + k:LCW + k + 1], lacc[:, :], ALU.mult, ALU.add,
                  reads=[t, prm, lacc], writes=[lacc])
        P.ts(lxc[:, :], lacc[:, :], prm[:, LCB:LCB + 1], None, ALU.add, reads=[lacc, prm], writes=[lxc])
        P.copy(lxb[:, :], lxc[:, :], reads=[lxc], writes=[lxb], eng="pool")
        pa_, pb_ = PS[0], PS[1]
        P.mm(pa_[:, 0:N], lwb[:, 0, :], lxb[:, :], True, True, reads=[lwb, lxb], writes=[pa_])
        P.mm(pb_[:, 0:N], lwb[:, 1, :], lxb[:, :], True, True, reads=[lwb, lxb], writes=[pb_])
        P.act(lrg[:, :], pa_[:, 0:N], AF.Sigmoid, bias=prm[:, BA:BA + 1], reads=[pa_, prm], writes=[lrg])
        P.act(lig[:, :], pb_[:, 0:N], AF.Sigmoid, bias=prm[:, BX:BX + 1], reads=[pb_, prm], writes=[lig])
        P.act(la[:, :], lrg[:, :], AF.Exp, scale=lc[:, 0:1], reads=[lrg, lc], writes=[la])
        P.act(lb[:, :], lrg[:, :], AF.Exp, scale=lc[:, 1:2], reads=[lrg, lc], writes=[lb])
        P.ts(lb[:, :], lb[:, :], -1.0, 1.0, ALU.mult, ALU.add, reads=[lb], writes=[lb])
        P.act(lb[:, :], lb[:, :], AF.Sqrt, reads=[lb], writes=[lb])
        P.tt(lb[:, :], lb[:, :], lig[:, :], ALU.mult, reads=[lb, lig], writes=[lb], eng="pool")
        P.tt(lb[:, :], lb[:, :], lxc[:, :], ALU.mult, reads=[lb, lxc], writes=[lb], eng="pool")
        P.op("dve", lambda e, o=lh, a=la, b=lb, h=hst: e.tensor_tensor_scan(o[:, :], a[:, :], b[:, :], h[:, 0:1], ALU.mult, ALU.add),
             reads=[la, lb, hst], writes=[lh])
        P.copy(hst[:, :], lh[:, N - 1:N], reads=[lh], writes=[hst])
        gt = cur["lg"]
        P.tt(lgt[:, :], gt[:, 0:N], gt[:, 0:N], ALU.mult, reads=[gt], writes=[lgt], eng="pool")
        P.ts(lgt[:, :], lgt[:, :], 0.044715, 1.0, ALU.mult, ALU.add, reads=[lgt], writes=[lgt], eng="pool")
        P.tt(lgt[:, :], lgt[:, :], gt[:, 0:N], ALU.mult, reads=[lgt, gt], writes=[lgt], eng="pool")
        P.act(lgt[:, :], lgt[:, :], AF.Sigmoid, scale=1.5957691216, reads=[lgt], writes=[lgt])
        P.tt(lgt[:, :], lgt[:, :], gt[:, 0:N], ALU.mult, reads=[lgt, gt], writes=[lgt], eng="pool")
        lo = lout[ti % 2]
        P.tt(lo[:, :], lgt[:, :], lh[:, :], ALU.mult, reads=[lgt, lh], writes=[lo], eng="pool")
        P.dma(ylru[:, t0:t0 + N], lo[:, :], reads=[lo])

        def shift(src, dst, mucol, nr=128):
            P.tt(tmp[0:nr, :], src[0:nr, 0:N], src[0:nr, 1:N + 1], ALU.subtract, reads=[src], writes=[tmp])
            P.stt(dst[0:nr, :], tmp[0:nr, :], prm[0:nr, mucol:mucol + 1], src[0:nr, 1:N + 1], ALU.mult, ALU.add,
                  reads=[tmp, prm, src], writes=[dst])
        shift(cur["r"], r_, MU + 0)
        shift(cur["k"], k_, MU + 1)
        shift(cur["v"], v_, MU + 2)
        shift(cur["wa"], wa_, MU + 3)
        shift(cur["gd"], gsh, MU + 5)
        P.act(twb[0:64, :], wa_[0:64, :], AF.Tanh, reads=[wa_], writes=[(twb, 0)])
        P.copy(twb[64:128, :], wa_[64:128, :], reads=[wa_], writes=[(twb, 1)], eng="pool")
        P.act(sgp[:, :, 64:128], gsh[:, :].rearrange("p (c t) -> p c t", t=CH), AF.Sigmoid, reads=[gsh], writes=[sgp])
        pw_, pa2 = PS[2], PS[3]
        P.mm(pw_[:, 0:N], wup[0:64, :], twb[0:64, :], True, True, reads=[wup, (twb, 0)], writes=[pw_])
        P.mm(pa2[:, 0:N], wup[64:128, :], twb[64:128, :], True, True, reads=[wup, (twb, 1)], writes=[pa2])
        P.act(lw_[:, :], pw_[:, 0:N], AF.Sigmoid, bias=prm[:, W0:W0 + 1], reads=[pw_, prm], writes=[lw_])
        P.ts(lw_[:, :], lw_[:, :], -0.6065306597126334, None, ALU.mult, reads=[lw_], writes=[lw_], eng="pool")
        P.act(a_[:, :], pa2[:, 0:N], AF.Sigmoid, bias=prm[:, A0:A0 + 1], reads=[pa2, prm], writes=[a_])
        P.ts(kk[:, :], k_[:, :], prm[:, KK_:KK_ + 1], None, ALU.mult, reads=[k_, prm], writes=[kk])
        P.tt(tmp[:, :], kk[:, :], kk[:, :], ALU.mult, reads=[kk], writes=[tmp], eng="pool")
        pn_ = PS[4]
        P.mm(pn_[:, 0:N], bones, tmp[:, :], True, True, reads=[cst, tmp], writes=[pn_])
        P.act(tmp[:, :], pn_[:, 0:N], AF.Ln, bias=C.cst(1e-12), reads=[pn_], writes=[tmp])
        P.act(tmp[:, :], tmp[:, :], AF.Exp, scale=-0.5, reads=[tmp], writes=[tmp])
        P.tt(kk[:, :], kk[:, :], tmp[:, :], ALU.mult, reads=[kk, tmp], writes=[kk])
        P.ts(kp[:, :], a_[:, :], prm[:, KA:KA + 1], omka[:, 0:1], ALU.mult, ALU.add, reads=[a_, prm, omka], writes=[kp])
        P.tt(kp[:, :], kp[:, :], k_[:, :], ALU.mult, reads=[kp, k_], writes=[kp], eng="pool")
        P.tt(b_[:, :], kk[:, :], a_[:, :], ALU.mult, reads=[kk, a_], writes=[b_], eng="pool")
        P.stt(prod[:, :], r_[:, :], prm[:, RK_:RK_ + 1], kp[:, :], ALU.mult, ALU.mult, reads=[r_, prm, kp], writes=[prod])
        P.op("dve", lambda e: e.tensor_tensor_scan(cl[:, :], rmask[:, :], lw_[:, :], 0.0, ALU.mult, ALU.add),
             reads=[rmask, lw_], writes=[cl])
        cl3 = cl[:, :].rearrange("p (c t) -> p c t", t=CH)
        P.act(Wq[:, :], cl[:, :], AF.Exp, reads=[cl], writes=[Wq])
        P.tt(tmp[:, :], cl[:, :], lw_[:, :], ALU.subtract, reads=[cl, lw_], writes=[tmp], eng="pool")
        P.act(Wkk[:, :], tmp[:, :], AF.Exp, reads=[tmp], writes=[Wkk])
        P.act(Winv[:, :], cl[:, :], AF.Exp, scale=-1.0, reads=[cl], writes=[Winv])
        P.tt(Wend[:, :].rearrange("p (c t) -> p c t", t=CH), cl3[:, :, CH - 1:CH].broadcast_to([128, NCH, CH]), cl3,
             ALU.subtract, reads=[cl], writes=[Wend])
        P.act(Wend[:, :], Wend[:, :], AF.Exp, reads=[Wend], writes=[Wend])
        P.act(WC[:, :], cl3[:, :, CH - 1], AF.Exp, reads=[cl], writes=[WC])
        ob = osb[ti % 2]
        for c in range(NCH):
            gi = ti * NCH + c
            cs = slice(c * CH, (c + 1) * CH)
            bt, kt, vb, bh, kh, pr = (bd[n][gi % NB] for n in ("bt", "kt", "vb", "bh", "kh", "pr"))
            RK = RKb[gi % NB]
            g2 = gi % 2
            M1, M2, VT, Rsb, SA, bhT, khT = M1s[g2], M2s[g2], VTs[g2], Rsbs[g2], SAs[g2], bhTs[g2], khTs[g2]
            for hh in range(2):
                ps_ = slice(hh * 64, hh * 64 + 64)
                e1 = "dve" if hh == 0 else "pool"
                P.tt(bt[ps_, ps_], b_[ps_, cs], Winv[ps_, cs], ALU.mult, reads=[b_, Winv], writes=[(bt, hh)], eng=e1)
                P.tt(kt[ps_, ps_], kp[ps_, cs], Winv[ps_, cs], ALU.mult, reads=[kp, Winv], writes=[(kt, hh)], eng=e1)
                P.tt(RK[ps_, hh * 64:hh * 64 + 64], kk[ps_, cs], Wkk[ps_, cs], ALU.mult, reads=[kk, Wkk], writes=[(RK, hh)], eng=e1)
                P.tt(RK[ps_, 128 + hh * 64:128 + hh * 64 + 64], r_[ps_, cs], Wq[ps_, cs], ALU.mult, reads=[r_, Wq], writes=[(RK, 2 + hh)], eng=e1)
                P.tt(bh[ps_, ps_], b_[ps_, cs], Wend[ps_, cs], ALU.mult, reads=[b_, Wend], writes=[(bh, hh)], eng=e1)
                P.tt(kh[ps_, ps_], kp[ps_, cs], Wend[ps_, cs], ALU.mult, reads=[kp, Wend], writes=[(kh, hh)], eng=e1)
                P.copy(vb[ps_, ps_], v_[ps_, cs], reads=[v_], writes=[(vb, hh)], eng="pool")
                P.copy(pr[ps_, ps_], prod[ps_, cs], reads=[prod], writes=[(pr, hh)], eng="pool")
            P.mm(PS[0][:, 0:256], bt[:, :], RK[:, :], True, True, reads=[bt, RK], writes=[PS[0]])
            P.mm(PS[1][:, 0:256], kt[:, :], RK[:, :], True, True, reads=[kt, RK], writes=[PS[1]])
            P.mm(PS[2][:, 0:128], RK[:, 0:128], bt[:, :], True, True, reads=[RK, bt], writes=[PS[2]])
            P.tt(M1[:, :], PS[0][:, 0:256], mask2[:, :], ALU.mult, reads=[PS[0], mask2], writes=[M1])
            P.tt(M2[:, :], PS[1][:, 0:256], mask2[:, :], ALU.mult, reads=[PS[1], mask2], writes=[M2])
            Pc, PTc = Pw[2 * g2], PTw[2 * g2]
            P.tt(PTc[:, :], PS[2][:, 0:128], m_sl, ALU.mult, reads=[PS[2], cst], writes=[PTc])
            P.copy(Pc[:, :], M1[:, 0:128], reads=[M1], writes=[Pc], eng="pool")
            X = Xs[2 * g2]
            P.tt(X[:, :], ident, M1[:, 0:128], ALU.subtract, reads=[cst, M1], writes=[X], eng="pool")
            P.mm(PS[3][:, 0:64], vb[:, :], isel[:, 0:64], True, True, reads=[vb, cst], writes=[PS[3]])
            P.copy(VT[:, :], PS[3][:, 0:64], reads=[PS[3]], writes=[VT], eng="act")
            P.transpose(PS[4][:, 0:128], bh[:, :], ident, reads=[bh, cst], writes=[PS[4]])
            P.transpose(PS[4][:, 128:256], kh[:, :], ident, reads=[kh, cst], writes=[PS[4]])
            P.copy(bhT[:, :], PS[4][:, 0:128], reads=[PS[4]], writes=[bhT], eng="act")
            P.copy(khT[:, :], PS[4][:, 128:256], reads=[PS[4]], writes=[khT], eng="act")
            P.mm(PS[3][:, 64:65], pr[:, :], ones1[:, :], True, True, reads=[pr, ones1], writes=[PS[3]])
            P.copy(st4[:, 4:5], PS[3][:, 64:65], reads=[PS[3]], writes=[(st4, 4)], eng="act")
            for j in range(1, 6):
                Pn, PTn, Xn = Pw[2 * g2 + j % 2], PTw[2 * g2 + j % 2], Xs[2 * g2 + j % 2]
                pp, pt_, px = PS[5], PS[6], PS[7]
                P.mm(pt_[:, 0:128], Pc[:, :], PTc[:, :], True, True, reads=[Pc, PTc], writes=[pt_])
                if j < 5:
                    P.mm(pp[:, 0:128], PTc[:, :], Pc[:, :], True, True, reads=[Pc, PTc], writes=[pp])
                P.copy(PTn[:, :], pt_[:, 0:128], reads=[pt_], writes=[PTn], eng="act")
                if j < 5:
                    P.copy(Pn[:, :], pp[:, 0:128], reads=[pp], writes=[Pn])
                P.mm(px[:, 0:128], PTn[:, :], X[:, :], True, True, reads=[PTn, X], writes=[px])
                P.tt(Xn[:, :], X[:, :], px[:, 0:128], ALU.add, reads=[X, px], writes=[Xn])
                Pc, PTc, X = Pn, PTn, Xn
            pr_ = PS[2]
            P.mm(pr_[:, 128:192], RK[:, 0:128], ST[:, :], True, False, reads=[RK, ST], writes=[pr_])
            P.mm(pr_[:, 128:192], M2[:, 0:128], VT[:, :], False, True, reads=[M2, VT], writes=[pr_])
            P.ts(Rsb[:, :], pr_[:, 128:192], -1.0, None, ALU.mult, reads=[pr_], writes=[Rsb])
            P.mm(pr_[:, 192:256], X[:, :], Rsb[:, :], True, True, reads=[X, Rsb], writes=[pr_])
            P.copy(SA[:, :], pr_[:, 192:256], reads=[pr_], writes=[SA])
            py = PS[3]
            P.mm(py[:, 128:192], RK[:, 128:256], ST[:, :], True, False, reads=[RK, ST], writes=[py])
            P.mm(py[:, 128:192], M1[:, 128:256], SA[:, :], False, False, reads=[M1, SA], writes=[py])
            P.mm(py[:, 128:192], M2[:, 128:256], VT[:, :], False, True, reads=[M2, VT], writes=[py])
            P.mm(py[:, 192:256], sgp[:, c, 64:192], gup[:, 0:64], True, False, reads=[sgp, gup], writes=[py])
            P.mm(py[:, 192:256], sgp[:, c, 0:128], gup[:, 64:128], False, True, reads=[sgp, gup], writes=[py])
            pst = PS[4]
            P.mm(pst[:, 256:320], bhT[:, :], SA[:, :], True, False, reads=[bhT, SA], writes=[pst])
            P.mm(pst[:, 256:320], khT[:, :], VT[:, :], False, True, reads=[khT, VT], writes=[pst])
            P.stt(ST[:, :], ST[:, :], WC[:, c:c + 1], pst[:, 256:320], ALU.mult, ALU.add, reads=[ST, WC, pst], writes=[ST])
            P.copy(ysb[:, :], py[:, 128:192], reads=[py], writes=[ysb], eng="act")
            P.op("dve", lambda e, o=st4, i=ysb: e.reduce_sum(o[:, 0:1], i[:, :], AX.X), reads=[ysb], writes=[(st4, 0)])
            P.ts(st4[:, 1:2], st4[:, 0:1], -1.0 / 64, None, ALU.mult, reads=[(st4, 0)], writes=[(st4, 1)])
            P.act(junk[:, :], ysb[:, :], AF.Square, bias=st4[:, 1:2], reads=[ysb, (st4, 1)], writes=[junk])
            P.op("dve", lambda e, o=st4, i=junk: e.reduce_sum(o[:, 2:3], i[:, :], AX.X), reads=[junk], writes=[(st4, 2)])
            P.act(st4[:, 3:4], st4[:, 2:3], AF.Ln, bias=C.cst(64e-5), scale=1.0 / 64, reads=[(st4, 2)], writes=[(st4, 3)])
            P.act(st4[:, 3:4], st4[:, 3:4], AF.Exp, scale=-0.5, reads=[(st4, 3)], writes=[(st4, 3)])
            P.ts(yn[:, :], ysb[:, :], st4[:, 1:2], st4[:, 3:4], ALU.add, ALU.mult, reads=[ysb, (st4, 1), (st4, 3)], writes=[yn])
            P.tt(yn[:, :], yn[:, :], lng[:, 0, :], ALU.mult, reads=[yn, lng], writes=[yn], eng="pool")
            P.tt(yn[:, :], yn[:, :], lng[:, 1, :], ALU.add, reads=[yn, lng], writes=[yn], eng="pool")
            P.stt(yn[:, :], VT[:, :], st4[:, 4:5], yn[:, :], ALU.mult, ALU.add, reads=[VT, (st4, 4), yn], writes=[yn])
            P.tt(ob[:, c, :], yn[:, :], py[:, 192:256], ALU.mult, reads=[yn, py], writes=[(ob, c)])
        for hh in range(2):
            P.dma(yrw[t0:t0 + N, hh * 64:(hh + 1) * 64].rearrange("(c t) v -> t c v", t=CH), ob[hh * 64:(hh + 1) * 64, :, :],
                  reads=[ob])
    return P.build()


def _consts_odd():
    i = np.arange(128)[:, None]
    j = np.arange(128)[None, :]
    same = (i // 64) == (j // 64)
    c = np.zeros((128, 6, 128), np.float32)
    c[:, 0, :] = (i == j)
    c[:, 1, :] = same & (j > i)
    c[:, 2, :] = same & (j < i)
    c[:, 3, :] = same & (j >= i)
    c[:, 4, :] = same
    c[:, 5, 0:64] = ((i % 64) == j[:, 0:64])
    return c


def pack_odd(pTb, q, S, prm, N=512):
    o = 128 * q
    rows = [pTb[o:o + 128], pTb[512 + o:512 + o + 128], pTb[1024 + o:1024 + o + 128], pTb[1536:1664],
            pTb[1664:1792], pTb[1792 + o:1792 + o + 128], pTb[2304 + o:2304 + o + 128]]
    pin = np.ascontiguousarray(np.concatenate(rows, 0))
    mu = prm["rwkv_mu"]
    pp = np.zeros((128, 24), np.float32)
    pp[:, 0] = mu[o:o + 128]
    pp[:, 1] = mu[512 + o:512 + o + 128]
    pp[:, 2] = mu[1024 + o:1024 + o + 128]
    pp[:, 3] = mu[1536:1664]
    pp[:, 5] = mu[1664:1792]
    pp[:, 6] = prm["rwkv_w0"][o:o + 128]
    pp[:, 7] = prm["rwkv_a0"][o:o + 128]
    pp[:, 8] = prm["rwkv_k_k"][o:o + 128]
    pp[:, 9] = prm["rwkv_k_a"][o:o + 128]
    pp[:, 10] = prm["rwkv_r_k"].reshape(512)[o:o + 128]
    pp[:, 11] = prm["lru_conv_b"][o:o + 128]
    pp[:, 12:16] = prm["lru_conv_w"][:, o:o + 128].T
    pp[:, 16] = prm["lru_ba"][o:o + 128]
    pp[:, 17] = prm["lru_bx"][o:o + 128]
    pp[:, 18] = prm["lru_lambda"][o:o + 128]
    wup = np.concatenate([prm["rwkv_w_up"][:, o:o + 128], prm["rwkv_a_up"][:, o:o + 128]], 0)
    gup = prm["rwkv_g_up"][:, o:o + 128]
    lruw = np.zeros((128, 2, 128), np.float32)
    for hh in range(2):
        lruw[hh * 64:hh * 64 + 64, 0, hh * 64:hh * 64 + 64] = prm["lru_wa"][2 * q + hh]
        lruw[hh * 64:hh * 64 + 64, 1, hh * 64:hh * 64 + 64] = prm["lru_wx"][2 * q + hh]
    lng = np.zeros((128, 2, 64), np.float32)
    for hh in range(2):
        lng[hh * 64:hh * 64 + 64, 0, :] = prm["rwkv_ln_g"][o + hh * 64:o + hh * 64 + 64][None, :]
        lng[hh * 64:hh * 64 + 64, 1, :] = prm["rwkv_ln_b"][o + hh * 64:o + hh * 64 + 64][None, :]
    rmask = np.ones((128, N), np.float32)
    rmask[:, ::64] = 0.0
    return {"pin": pin, "prm": pp, "wup": np.ascontiguousarray(wup), "gup": np.ascontiguousarray(gup), "lruw": lruw,
            "lng": lng, "cst": _consts_odd(), "rmask": rmask}


_PROGS = {}


def _prog(key, fn):
    if key not in _PROGS:
        _PROGS[key] = fn()
    return _PROGS[key]


def _run(nc, ims):
    res = run_bass_kernel_spmd(nc, ims, core_ids=list(range(NCORES)))
    return res.results


def _lay(v):
    return np.ascontiguousarray(np.asarray(v, np.float32).reshape(8, 128).T)


def kernel(**inp):
    inp = {k: np.asarray(v) for k, v in inp.items()}
    x = inp["x"].astype(np.float32, copy=False)
    QT = SEQ // 4
    cores = [(c // 4, c % 4) for c in range(NCORES)]
    hT = [np.ascontiguousarray(x[b, qq * QT:(qq + 1) * QT, :].T) for (b, qq) in cores]
    ev_names = ("pool_w", "pool_scale", "ssd_conv_w", "ssd_conv_b", "ssd_dt_bias", "ssd_a_log", "ssd_d", "ssd_norm_g")
    od_names = ("rwkv_mu", "rwkv_w0", "rwkv_w_up", "rwkv_a0", "rwkv_a_up", "rwkv_g_up", "rwkv_k_k", "rwkv_k_a", "rwkv_r_k",
                "rwkv_ln_g", "rwkv_ln_b", "lru_conv_w", "lru_conv_b", "lru_wa", "lru_ba", "lru_wx", "lru_bx", "lru_lambda")
    ncA = _prog(("A", EVEN_IN), lambda: build_A(EVEN_IN))
    g0 = _lay(inp["mix_norm_g"][0])
    w0 = np.ascontiguousarray(inp["ev_w_in"][0])
    res = _run(ncA, [{"hT": hT[c], "g": g0, "w_in": w0} for c in range(NCORES)])
    pT = [r["pT"] for r in res]
    outT = None
    for layer in range(DEPTH):
        i = layer // 2
        even = layer % 2 == 0
        pTb = [np.concatenate([pT[b * 4 + qq] for qq in range(4)], axis=1) for b in range(BATCH)]
        if even:
            prm = {k: inp[k][i] for k in ev_names}
            ncM = _prog("Meven", build_Meven)
            res = _run(ncM, [pack_even(pTb[b], qq, SEQ, prm) for (b, qq) in cores])
            yT = [np.ascontiguousarray(np.concatenate(
                [res[c]["ypool"]] + [res[b * 4 + q2]["yssd"][:, qq * QT:(qq + 1) * QT] for q2 in range(4)], axis=0))
                for c, (b, qq) in enumerate(cores)]
        else:
            prm = {k: inp[k][i] for k in od_names}
            ncM = _prog("Modd", build_Modd)
            res = _run(ncM, [pack_odd(pTb[b], qq, SEQ, prm) for (b, qq) in cores])
            yT = [np.ascontiguousarray(np.concatenate(
                [res[b * 4 + q2]["yrw"][qq * QT:(qq + 1) * QT].T for q2 in range(4)] +
                [res[b * 4 + q2]["ylru"][:, qq * QT:(qq + 1) * QT] for q2 in range(4)], axis=0))
                for c, (b, qq) in enumerate(cores)]
        del pTb
        ncC1 = _prog(("C1", even), lambda: build_C1(even))
        common = {"g": _lay(inp["ffn_norm_g"][layer]),
                  "w_out": np.ascontiguousarray((inp["ev_w_out"] if even else inp["od_w_out"])[i]),
                  "w_gate": np.ascontiguousarray(inp["ffn_w_gate"][layer]),
                  "w_up": np.ascontiguousarray(inp["ffn_w_up"][layer])}
        if even:
            common["ssd_g"] = _lay(np.concatenate([np.ones(256, np.float32), inp["ssd_norm_g"][i]]))
        res = _run(ncC1, [dict(common, hT=hT[c], yT=yT[c]) for c in range(NCORES)])
        h1T = [r["h1T"] for r in res]
        actT = [r["actT"] for r in res]
        del yT
        wd = np.ascontiguousarray(inp["ffn_w_down"][layer])
        if layer < DEPTH - 1:
            nxt_even = (layer + 1) % 2 == 0
            F = EVEN_IN if nxt_even else ODD_IN
            w_in = np.ascontiguousarray((inp["ev_w_in"] if nxt_even else inp["od_w_in"])[(layer + 1) // 2])
            ncC2 = _prog(("C2", F), lambda: build_C2(F, False))
            gn = _lay(inp["mix_norm_g"][layer + 1])
            res = _run(ncC2, [{"hT": h1T[c], "actT": actT[c], "g": gn, "w_down": wd, "w_in": w_in} for c in range(NCORES)])
            hT = [r["h2T"] for r in res]
            pT = [r["pT"] for r in res]
        else:
            ncC2 = _prog(("C2", "final"), lambda: build_C2(0, True))
            gn = _lay(inp["final_norm_g"])
            res = _run(ncC2, [{"hT": h1T[c], "actT": actT[c], "g": gn, "w_down": wd} for c in range(NCORES)])
            outT = [r["outT"] for r in res]
    out = np.empty((BATCH, SEQ, D), np.float32)
    for c, (b, qq) in enumerate(cores):
        out[b, qq * QT:(qq + 1) * QT, :] = outT[c].T
    return out
```
